# Optimizing a Trainium2 kernel written in Bass

```python
import jax, jax.numpy as jnp
from jax import lax
import numpy as np

D_MODEL = 1024
BATCH = 8
SEQ = 2048
DEPTH = 2
DEC_BATCH = 32
DEC_SEQ = 1
PAST_LEN = 8192
PAGE_SIZE = 128

HEAD_DIM = 64
D_MIX = D_MODEL
H_FOX = 6
H_NSA = 6
KV_NSA = 2
G_NSA = H_NSA // KV_NSA
POOL_WINDOWS = (2, 4, 8, 16)
N_POOL_GROUPS = len(POOL_WINDOWS)
C_POOL = D_MIX - (H_FOX + H_NSA) * HEAD_DIM
POOL_GW = C_POOL // N_POOL_GROUPS
POOL_STATE = max(POOL_WINDOWS) - 1
ROT_DIM = HEAD_DIM // 4
ROPE_THETA = 500000.0
Q_BLOCK = 128
CMP_BLOCK = 64
SEL_BLOCK = CMP_BLOCK
TOPK_BLOCKS = 8
WINDOW = 512
D_FF = 4 * D_MODEL
EPS = 1e-6
NEG = -1e30
FORCE = 1e4
SCALE = HEAD_DIM ** -0.5
_IN_SIZES = (H_FOX * HEAD_DIM, H_FOX * HEAD_DIM, H_FOX * HEAD_DIM, H_FOX,
             H_NSA * HEAD_DIM,
             KV_NSA * HEAD_DIM, KV_NSA * HEAD_DIM, KV_NSA * HEAD_DIM,
             KV_NSA * HEAD_DIM, KV_NSA * HEAD_DIM, KV_NSA * HEAD_DIM,
             3 * H_NSA,
             C_POOL)
N_IN = sum(_IN_SIZES)

kernel_name = 'hybrid_fox_nsa_pool_decoder'

f32 = jnp.float32


def _rms(x, g):
    xf = x.astype(f32)
    y = xf * lax.rsqrt(jnp.mean(jnp.square(xf), axis=-1, keepdims=True) + EPS)
    return (y * g.astype(f32)).astype(x.dtype)


def _masked_softmax(s, mask):
    s = jnp.where(mask, s.astype(f32), NEG)
    p = jax.nn.softmax(s, axis=-1)
    return jnp.where(mask, p, 0.0)


def _rope(x, pos):
    half = ROT_DIM // 2
    inv = ROPE_THETA ** (-jnp.arange(0, ROT_DIM, 2, dtype=f32) / ROT_DIM)
    ang = pos.astype(f32)[:, None] * inv[None, :]
    cos = jnp.cos(ang)[:, None, :].astype(x.dtype)
    sin = jnp.sin(ang)[:, None, :].astype(x.dtype)
    x1, x2 = x[..., :half], x[..., half:ROT_DIM]
    return jnp.concatenate([x1 * cos - x2 * sin, x2 * cos + x1 * sin, x[..., ROT_DIM:]], axis=-1)


def _split_in(z):
    cuts = np.cumsum(_IN_SIZES)[:-1].tolist()
    return jnp.split(z, cuts, axis=-1)


def _heads(t, n):
    return t.reshape(t.shape[0], t.shape[1], n, HEAD_DIM)


def _fox_attend(q, cq, qpos, k, v, ck):
    tk = k.shape[1]
    s = jnp.einsum('bqhd,bkhd->bhqk', q, k).astype(f32) * SCALE
    s = s + (jnp.transpose(cq, (0, 2, 1))[..., :, None] - jnp.transpose(ck, (0, 2, 1))[..., None, :])
    mask = jnp.arange(tk)[None, :] <= qpos[:, None]
    p = _masked_softmax(s, mask)
    return jnp.einsum('bhqk,bkhd->bqhd', p.astype(v.dtype), v)


def _fox_prompt(q, k, v, logf):
    b, t = q.shape[:2]
    nb = t // Q_BLOCK
    c = jnp.cumsum(logf.astype(f32), axis=1)
    qb = jnp.moveaxis(q.reshape(b, nb, Q_BLOCK, H_FOX, HEAD_DIM), 1, 0)
    cqb = jnp.moveaxis(c.reshape(b, nb, Q_BLOCK, H_FOX), 1, 0)
    pos = jnp.arange(t).reshape(nb, Q_BLOCK)
    o = lax.map(lambda a: _fox_attend(a[0], a[1], a[2], k, v, c), (qb, cqb, pos))
    return jnp.moveaxis(o, 0, 1).reshape(b, t, H_FOX, HEAD_DIM)


def _cmp_branch(q5, qpos, kc_rows, vc_rows):
    b, l = kc_rows.shape[:2]
    nbc = l // CMP_BLOCK
    def blocks(r):
        rb = r[:, :nbc * CMP_BLOCK].reshape(b, nbc, CMP_BLOCK, KV_NSA, HEAD_DIM)
        return jnp.mean(rb.astype(f32), axis=2).astype(r.dtype)
    kc, vc = blocks(kc_rows), blocks(vc_rows)
    s = jnp.einsum('btkgd,bnkd->btkgn', q5, kc) * SCALE
    blk_end = (jnp.arange(nbc) + 1) * CMP_BLOCK - 1
    mask = (blk_end[None, :] <= qpos[:, None])[None, :, None, None, :]
    p = _masked_softmax(s, mask)
    o = jnp.einsum('btkgn,bnkd->btkgd', p.astype(vc.dtype), vc)
    return o, jnp.sum(p, axis=3)


def _select_blocks(imp, qpos, nbs):
    nbc = imp.shape[-1]
    imp = jnp.pad(imp, ((0, 0), (0, 0), (0, 0), (0, nbs - nbc)))
    j = jnp.arange(nbs)[None, :]
    cur = (qpos // SEL_BLOCK)[:, None]
    forced = (j == 0) | (j == cur) | (j == cur - 1)
    valid = j * SEL_BLOCK <= qpos[:, None]
    score = jnp.where(forced[None, :, None, :], FORCE, imp)
    score = jnp.where(valid[None, :, None, :], score, -1.0)
    _, idx = lax.top_k(score, min(TOPK_BLOCKS, nbs))
    return idx


def _flat_blocks(ks, vs, kpos):
    b, tq, kv, nk, sb, hd = ks.shape
    return (ks.reshape(b, tq, kv, nk * sb, hd), vs.reshape(b, tq, kv, nk * sb, hd),
            kpos.reshape(b, tq, kv, nk * sb))


def _gather_blocks(kb, vb, idx):
    bi = jnp.arange(idx.shape[0])[:, None, None, None]
    hk = jnp.arange(KV_NSA)[None, None, :, None]
    kpos = idx[..., None] * SEL_BLOCK + jnp.arange(SEL_BLOCK)
    return _flat_blocks(kb[bi, idx, :, hk], vb[bi, idx, :, hk], kpos)


def _gather_paged_blocks(cache_nsa_kv, l, page_table, k_new, v_new, idx, past):
    b = idx.shape[0]
    n_past_blk = past // SEL_BLOCK
    per_page = PAGE_SIZE // SEL_BLOCK
    bi = jnp.arange(b)[:, None, None, None]
    hk = jnp.arange(KV_NSA)[None, None, :, None]
    off = jnp.arange(SEL_BLOCK)
    jp = jnp.minimum(idx, n_past_blk - 1)
    phys = page_table[bi, jp // per_page][..., None]
    within = (jp % per_page)[..., None] * SEL_BLOCK + off
    k_old = cache_nsa_kv[l, phys, within, 2, hk[..., None]]
    v_old = cache_nsa_kv[l, phys, within, 3, hk[..., None]]
    tn = k_new.shape[1]
    n_new_blk = -(-tn // SEL_BLOCK)
    padw = ((0, 0), (0, n_new_blk * SEL_BLOCK - tn), (0, 0), (0, 0))
    kn = jnp.pad(k_new, padw).reshape(b, n_new_blk, SEL_BLOCK, KV_NSA, HEAD_DIM)
    vn = jnp.pad(v_new, padw).reshape(b, n_new_blk, SEL_BLOCK, KV_NSA, HEAD_DIM)
    jn = jnp.clip(idx - n_past_blk, 0, n_new_blk - 1)
    is_old = (idx < n_past_blk)[..., None, None]
    ks = jnp.where(is_old, k_old, kn[bi, jn, :, hk])
    vs = jnp.where(is_old, v_old, vn[bi, jn, :, hk])
    kpos = idx[..., None] * SEL_BLOCK + off
    return _flat_blocks(ks, vs, kpos)


def _sel_attend(q5, qpos, ks, vs, kpos):
    s = jnp.einsum('btkgd,btknd->btkgn', q5, ks) * SCALE
    mask = (kpos <= qpos[None, :, None, None])[:, :, :, None, :]
    p = _masked_softmax(s, mask)
    return jnp.einsum('btkgn,btknd->btkgd', p.astype(vs.dtype), vs)


def _win_attend(q5, qpos, kw, vw, kpos):
    s = jnp.einsum('btkgd,bnkd->btkgn', q5, kw) * SCALE
    rel = qpos[:, None] - kpos[None, :]
    mask = (rel >= 0) & (rel < WINDOW) & (kpos[None, :] >= 0)
    p = _masked_softmax(s, mask[None, :, None, None, :])
    return jnp.einsum('btkgn,bnkd->btkgd', p.astype(vw.dtype), vw)


def _nsa_combine(o_cmp, o_sel, o_win, gate_logits):
    b, t = gate_logits.shape[:2]
    g = jax.nn.sigmoid(gate_logits.astype(f32)).reshape(b, t, KV_NSA, G_NSA, 3)
    o = (g[..., 0:1] * o_cmp.astype(f32) + g[..., 1:2] * o_sel.astype(f32)
         + g[..., 2:3] * o_win.astype(f32))
    return o.reshape(b, t, H_NSA * HEAD_DIM).astype(o_cmp.dtype)


def _nsa_prompt(q, kc, vc, ks, vs, kw, vw, gates):
    b, t = q.shape[:2]
    pos = jnp.arange(t)
    q5 = q.reshape(b, t, KV_NSA, G_NSA, HEAD_DIM)
    o_cmp, imp = _cmp_branch(q5, pos, kc, vc)
    nbs = t // SEL_BLOCK
    idx = _select_blocks(imp, pos, nbs)
    ksb = ks.reshape(b, nbs, SEL_BLOCK, KV_NSA, HEAD_DIM)
    vsb = vs.reshape(b, nbs, SEL_BLOCK, KV_NSA, HEAD_DIM)
    padw = ((0, 0), (WINDOW, 0), (0, 0), (0, 0))
    kwp, vwp = jnp.pad(kw, padw), jnp.pad(vw, padw)

    def one_block(s0):
        qb = lax.dynamic_slice_in_dim(q5, s0, Q_BLOCK, axis=1)
        qp = s0 + jnp.arange(Q_BLOCK)
        ib = lax.dynamic_slice_in_dim(idx, s0, Q_BLOCK, axis=1)
        ksg, vsg, kp = _gather_blocks(ksb, vsb, ib)
        o_sel = _sel_attend(qb, qp, ksg, vsg, kp)
        kwb = lax.dynamic_slice_in_dim(kwp, s0, WINDOW + Q_BLOCK, axis=1)
        vwb = lax.dynamic_slice_in_dim(vwp, s0, WINDOW + Q_BLOCK, axis=1)
        wp = s0 - WINDOW + jnp.arange(WINDOW + Q_BLOCK)
        o_win = _win_attend(qb, qp, kwb, vwb, wp)
        return o_sel, o_win

    o_sel, o_win = lax.map(one_block, jnp.arange(t // Q_BLOCK) * Q_BLOCK)
    unblock = lambda o: jnp.moveaxis(o, 0, 1).reshape(b, t, KV_NSA, G_NSA, HEAD_DIM)
    return _nsa_combine(o_cmp, unblock(o_sel), unblock(o_win), gates)


def _pool_mix(u_ext, n_prev, start_pos, w_pool, pool_scale):
    b, l, _ = u_ext.shape
    tn = l - n_prev
    cs = jnp.pad(jnp.cumsum(u_ext.astype(f32), axis=1), ((0, 0), (1, 0), (0, 0)))
    pos = start_pos + jnp.arange(tn)
    outs = []
    for g, w in enumerate(POOL_WINDOWS):
        csg = cs[:, :, g * POOL_GW:(g + 1) * POOL_GW]
        hi = csg[:, n_prev + 1:]
        lo = jnp.pad(csg, ((0, 0), (w, 0), (0, 0)))[:, n_prev + 1:n_prev + 1 + tn]
        cnt = jnp.minimum(w, pos + 1).astype(f32)[None, :, None]
        outs.append((hi - lo) / cnt - u_ext[:, n_prev:, g * POOL_GW:(g + 1) * POOL_GW].astype(f32))
    d = jnp.stack(outs, axis=2).astype(u_ext.dtype)
    y = jnp.einsum('btgc,gcd->btgd', d, w_pool).reshape(b, tn, C_POOL)
    return y * pool_scale


def _mixer_prompt(h, w_in, b_fox_f, w_out, w_pool, pool_scale):
    b, t, _ = h.shape
    fq, fk, fv, ff, nq, nkc, nvc, nks, nvs, nkw, nvw, ng, u = _split_in(h @ w_in)
    fq, fk, fv = _heads(fq, H_FOX), _heads(fk, H_FOX), _heads(fv, H_FOX)
    logf = jax.nn.log_sigmoid(ff.astype(f32) + b_fox_f.astype(f32))
    o_fox = _fox_prompt(fq, fk, fv, logf)
    pos = jnp.arange(t)
    nq = _rope(_heads(nq, H_NSA), pos)
    nkc, nvc = _rope(_heads(nkc, KV_NSA), pos), _heads(nvc, KV_NSA)
    nks, nvs = _rope(_heads(nks, KV_NSA), pos), _heads(nvs, KV_NSA)
    nkw, nvw = _rope(_heads(nkw, KV_NSA), pos), _heads(nvw, KV_NSA)
    o_nsa = _nsa_prompt(nq, nkc, nvc, nks, nvs, nkw, nvw, ng)
    o_pool = _pool_mix(u, 0, 0, w_pool, pool_scale)
    o = jnp.concatenate([o_fox.reshape(b, t, H_FOX * HEAD_DIM), o_nsa, o_pool], axis=-1) @ w_out
    wl = min(WINDOW, t)
    states = (jnp.stack([fk, fv], axis=2), logf, jnp.stack([nkc, nvc, nks, nvs], axis=2),
              jnp.stack([nkw, nvw], axis=2)[:, t - wl:], u[:, t - POOL_STATE:])
    return o, states


def _mixer_sample(h, l, cache_fox_kv, cache_fox_logf, cache_nsa_kv, cache_nsa_win, state_pool, page_table,
                  w_in, b_fox_f, w_out, w_pool, pool_scale):
    b, tn, _ = h.shape
    past = page_table.shape[1] * PAGE_SIZE
    qpos = past + jnp.arange(tn)
    fq, fk, fv, ff, nq, nkc, nvc, nks, nvs, nkw, nvw, ng, u = _split_in(h @ w_in)
    fq, fk, fv = _heads(fq, H_FOX), _heads(fk, H_FOX), _heads(fv, H_FOX)
    logf = jax.nn.log_sigmoid(ff.astype(f32) + b_fox_f.astype(f32))
    k_all = jnp.concatenate([cache_fox_kv[l, page_table, :, 0].reshape(b, past, H_FOX, HEAD_DIM), fk], axis=1)
    v_all = jnp.concatenate([cache_fox_kv[l, page_table, :, 1].reshape(b, past, H_FOX, HEAD_DIM), fv], axis=1)
    lf_all = jnp.concatenate([cache_fox_logf[l, page_table].reshape(b, past, H_FOX).astype(f32), logf], axis=1)
    c_all = jnp.cumsum(lf_all, axis=1)
    o_fox = _fox_attend(fq, c_all[:, past:], qpos, k_all, v_all, c_all)
    nq = _rope(_heads(nq, H_NSA), qpos)
    nkc, nvc = _rope(_heads(nkc, KV_NSA), qpos), _heads(nvc, KV_NSA)
    nks, nvs = _rope(_heads(nks, KV_NSA), qpos), _heads(nvs, KV_NSA)
    nkw, nvw = _rope(_heads(nkw, KV_NSA), qpos), _heads(nvw, KV_NSA)
    q5 = nq.reshape(b, tn, KV_NSA, G_NSA, HEAD_DIM)
    kc_all = jnp.concatenate([cache_nsa_kv[l, page_table, :, 0].reshape(b, past, KV_NSA, HEAD_DIM), nkc], axis=1)
    vc_all = jnp.concatenate([cache_nsa_kv[l, page_table, :, 1].reshape(b, past, KV_NSA, HEAD_DIM), nvc], axis=1)
    o_cmp, imp = _cmp_branch(q5, qpos, kc_all, vc_all)
    idx = _select_blocks(imp, qpos, -(-(past + tn) // SEL_BLOCK))
    ksg, vsg, kp = _gather_paged_blocks(cache_nsa_kv, l, page_table, nks, nvs, idx, past)
    o_sel = _sel_attend(q5, qpos, ksg, vsg, kp)
    win = cache_nsa_win[l]
    wb = win.shape[1]
    kw_all = jnp.concatenate([win[:, :, 0], nkw], axis=1)
    vw_all = jnp.concatenate([win[:, :, 1], nvw], axis=1)
    o_win = _win_attend(q5, qpos, kw_all, vw_all, past - wb + jnp.arange(wb + tn))
    o_nsa = _nsa_combine(o_cmp, o_sel, o_win, ng)
    u_ext = jnp.concatenate([state_pool[l], u], axis=1)
    o_pool = _pool_mix(u_ext, state_pool.shape[2], past, w_pool, pool_scale)
    o = jnp.concatenate([o_fox.reshape(b, tn, H_FOX * HEAD_DIM), o_nsa, o_pool], axis=-1) @ w_out
    states = (jnp.stack([fk, fv], axis=2), logf, jnp.stack([nkc, nvc, nks, nvs], axis=2),
              jnp.stack([kw_all, vw_all], axis=2)[:, tn:], u_ext[:, tn:])
    return o, states


def _sq_relu_mlp(h, w1, w2):
    a = jax.nn.relu(h @ w1)
    return (a * a) @ w2


def _layer(x, c, g, ada_w, ada_b, w1, w2, mixer):
    mod = (jax.nn.silu(c) @ ada_w + ada_b)[:, None, :]
    shift1, scale1, gate1, shift2, scale2, gate2 = jnp.split(mod, 6, axis=-1)
    h = _rms(x, g[0]) * (1.0 + scale1) + shift1
    o, states = mixer(h)
    x = x + gate1 * _rms(o, g[1])
    h = _rms(x, g[2]) * (1.0 + scale2) + shift2
    x = x + gate2 * _rms(_sq_relu_mlp(h, w1, w2), g[3])
    return x, states


def setup_inputs(seed: int = 0) -> dict:
    key = jax.random.key(seed)
    ks = jax.random.split(key, 20)
    n_pages = PAST_LEN // PAGE_SIZE
    n_used = DEC_BATCH * n_pages
    n_pool = n_used + max(1, n_used // 4)
    wbuf = min(WINDOW, PAST_LEN)
    nrm = lambda k, shape, s=1.0: s * jax.random.normal(k, shape, jnp.float32)
    x_prompt = nrm(ks[0], (BATCH, SEQ, D_MODEL))
    x_sample = nrm(ks[1], (DEC_BATCH, DEC_SEQ, D_MODEL))
    cache_fox_kv = nrm(ks[2], (DEPTH, n_pool, PAGE_SIZE, 2, H_FOX, HEAD_DIM))
    cache_fox_logf = jax.nn.log_sigmoid(nrm(ks[3], (DEPTH, n_pool, PAGE_SIZE, H_FOX), 0.5) + 3.0)
    cache_nsa_kv = nrm(ks[4], (DEPTH, n_pool, PAGE_SIZE, 4, KV_NSA, HEAD_DIM))
    cache_nsa_win = nrm(ks[5], (DEPTH, DEC_BATCH, wbuf, 2, KV_NSA, HEAD_DIM))
    state_pool = nrm(ks[6], (DEPTH, DEC_BATCH, POOL_STATE, C_POOL))
    page_table = jax.random.permutation(ks[7], n_pool)[:n_used].reshape(DEC_BATCH, n_pages).astype(jnp.int32)
    c_prompt = nrm(ks[8], (BATCH, D_MODEL))
    c_sample = nrm(ks[9], (DEC_BATCH, D_MODEL))
    w_ada = nrm(ks[10], (DEPTH, D_MODEL, 6 * D_MODEL), 0.5 * D_MODEL ** -0.5)
    b_ada = nrm(ks[11], (DEPTH, 6 * D_MODEL), 0.01)
    norm_g = 1.0 + nrm(ks[12], (DEPTH, 4, D_MODEL), 0.05)
    w_in = nrm(ks[13], (DEPTH, D_MODEL, N_IN), D_MODEL ** -0.5)
    b_fox_f = 3.0 + nrm(ks[14], (DEPTH, H_FOX), 0.5)
    w_out = nrm(ks[15], (DEPTH, D_MIX, D_MODEL), D_MIX ** -0.5)
    w_pool = nrm(ks[16], (DEPTH, N_POOL_GROUPS, POOL_GW, POOL_GW), POOL_GW ** -0.5)
    pool_scale = 1.0 + nrm(ks[17], (DEPTH, C_POOL), 0.1)
    w_ff1 = nrm(ks[18], (DEPTH, D_MODEL, D_FF), D_MODEL ** -0.5)
    w_ff2 = nrm(ks[19], (DEPTH, D_FF, D_MODEL), D_FF ** -0.5)
    return {'x_prompt': x_prompt, 'x_sample': x_sample,
            'cache_fox_kv': cache_fox_kv, 'cache_fox_logf': cache_fox_logf,
            'cache_nsa_kv': cache_nsa_kv, 'cache_nsa_win': cache_nsa_win, 'state_pool': state_pool,
            'page_table': page_table, 'c_prompt': c_prompt, 'c_sample': c_sample,
            'w_ada': w_ada, 'b_ada': b_ada, 'norm_g': norm_g, 'w_in': w_in, 'b_fox_f': b_fox_f,
            'w_out': w_out, 'w_pool': w_pool, 'pool_scale': pool_scale, 'w_ff1': w_ff1, 'w_ff2': w_ff2}


def reference(x_prompt, x_sample, cache_fox_kv, cache_fox_logf, cache_nsa_kv, cache_nsa_win, state_pool,
              page_table, c_prompt, c_sample, w_ada, b_ada, norm_g, w_in, b_fox_f, w_out, w_pool,
              pool_scale, w_ff1, w_ff2):
    yp, ys = x_prompt, x_sample
    sp, ss = [], []
    for l in range(DEPTH):
        mix_p = lambda h, l=l: _mixer_prompt(h, w_in[l], b_fox_f[l], w_out[l], w_pool[l], pool_scale[l])
        yp, st = _layer(yp, c_prompt, norm_g[l], w_ada[l], b_ada[l], w_ff1[l], w_ff2[l], mix_p)
        sp.append(st)
        mix_s = lambda h, l=l: _mixer_sample(h, l, cache_fox_kv, cache_fox_logf, cache_nsa_kv, cache_nsa_win,
                                             state_pool, page_table, w_in[l], b_fox_f[l], w_out[l],
                                             w_pool[l], pool_scale[l])
        ys, st = _layer(ys, c_sample, norm_g[l], w_ada[l], b_ada[l], w_ff1[l], w_ff2[l], mix_s)
        ss.append(st)
    stk = lambda lst, i: jnp.stack([s[i] for s in lst], axis=0)
    return (yp, ys,
            stk(sp, 0), stk(sp, 1), stk(sp, 2), stk(sp, 3), stk(sp, 4),
            stk(ss, 0), stk(ss, 1), stk(ss, 2), stk(ss, 3), stk(ss, 4))
```

```python
import numpy as np
from contextlib import ExitStack
import concourse.bass as bass
import concourse.mybir as mybir
from concourse.bass_utils import run_bass_kernel_spmd

F32 = mybir.dt.float32
BF16 = mybir.dt.bfloat16
I32 = mybir.dt.int32
AF = mybir.ActivationFunctionType
ALU = mybir.AluOpType
AX = mybir.AxisListType

D = 1024
T = 2048
NT = 16
NTT = 17
NS = 4
DEPTH = 2
HD = 64
NIN = 2584
DFF = 4096
SCALE = HD ** -0.5
EPS = 1e-6
NEGB = -30000.0
NPAGE = 64
G_OFF = [0, 384, 768, 1158, 1542, 2054, 2328]
G_SZ = [384, 384, 390, 384, 512, 274, 256]


class Trk:
    __slots__ = ("w", "r")

    def __init__(self):
        self.w = None
        self.r = {}


class V:
    __slots__ = ("ap", "trks")

    def __init__(self, ap, trks):
        self.ap = ap
        self.trks = trks

    def __getitem__(self, idx):
        return V(self.ap[idx], self.trks)

    def re(self, pat, **kw):
        return V(self.ap.rearrange(pat, **kw), self.trks)


class Buf:
    def __init__(self, t, nparts=1):
        self.t = t
        self.trks = [Trk() for _ in range(nparts)]

    def __getitem__(self, idx):
        return V(self.t[idx], self.trks)

    def p(self, i):
        return _Part(self, i)


class _Part:
    def __init__(self, b, i):
        self.b = b
        self.i = i

    def __getitem__(self, idx):
        return V(self.b.t[idx], [self.b.trks[self.i]])


class Ctx:
    def __init__(self, nc, es):
        self.nc = nc
        self.es = es
        self.eng = {"pe": nc.tensor, "act": nc.scalar, "dve": nc.vector, "pool": nc.gpsimd, "sp": nc.sync}
        self.sem = {}
        self.cnt = {}
        for k in ("pe", "act", "dve", "pool"):
            self.sem[k] = es.enter_context(nc.semaphore("pg_" + k))
            self.cnt[k] = 0
        self.nslot = {"sp": 12, "pool": 12, "act": 4}
        self.slot_i = {"sp": 0, "pool": 0, "act": 0}
        for q in ("sp", "pool", "act"):
            for i in range(self.nslot[q]):
                key = ("d", q, i)
                self.sem[key] = es.enter_context(nc.semaphore("dq_%s_%d" % (q, i)))
                self.cnt[key] = 0
        self.seen = {e: {} for e in self.eng}
        self.n_ins = 0

    def _need(self, R, W):
        need = {}
        for v in R:
            if v is None:
                continue
            for t in v.trks:
                if t.w is not None:
                    k, c = t.w
                    if need.get(k, 0) < c:
                        need[k] = c
        for v in W:
            for t in v.trks:
                if t.w is not None:
                    k, c = t.w
                    if need.get(k, 0) < c:
                        need[k] = c
                for k, c in t.r.items():
                    if need.get(k, 0) < c:
                        need[k] = c
        return need

    def _wait(self, e, need):
        seen = self.seen[e]
        for k, c in need.items():
            if k == e and e == "pe":
                continue
            if seen.get(k, 0) < c:
                self.eng[e].wait_ge(self.sem[k], c)
                seen[k] = c

    def _mark(self, R, W, tok):
        k, c = tok
        for v in R:
            if v is None:
                continue
            for t in v.trks:
                if t.r.get(k, 0) < c:
                    t.r[k] = c
        for v in W:
            for t in v.trks:
                t.w = tok
                t.r = {}

    def op(self, e, fn, R=(), W=(), inc=True):
        self._wait(e, self._need(R, W))
        ins = fn()
        self.n_ins += 1
        if inc:
            self.cnt[e] += 1
            ins.then_inc(self.sem[e], 1)
            tok = (e, self.cnt[e])
        else:
            tok = (e, self.cnt[e] + 1)
        self._mark(R, W, tok)
        return ins

    def dma(self, q, out, in_, R=(), W=()):
        R = list(R)
        W = list(W)
        if isinstance(in_, V):
            R.append(in_)
            in_ap = in_.ap
        else:
            in_ap = in_
        if isinstance(out, V):
            W.append(out)
            out_ap = out.ap
        else:
            out_ap = out
        i = self.slot_i[q]
        self.slot_i[q] = (i + 1) % self.nslot[q]
        key = ("d", q, i)
        need = self._need(R, W)
        if self.cnt[key] > 0:
            need[key] = max(need.get(key, 0), self.cnt[key])
        self._wait(q, need)
        self.cnt[key] += 16
        self.eng[q].dma_start(out=out_ap, in_=in_ap).then_inc(self.sem[key], 16)
        self.n_ins += 1
        self._mark(R, W, (key, self.cnt[key]))

    def idma(self, out, in_ap, idx):
        q = "pool"
        i = self.slot_i[q]
        self.slot_i[q] = (i + 1) % self.nslot[q]
        key = ("d", q, i)
        need = self._need([idx], [out])
        if self.cnt[key] > 0:
            need[key] = max(need.get(key, 0), self.cnt[key])
        self._wait(q, need)
        self.cnt[key] += 16
        self.nc.gpsimd.indirect_dma_start(out=out.ap, out_offset=None, in_=in_ap,
                                          in_offset=bass.IndirectOffsetOnAxis(ap=idx.ap, axis=0)).then_inc(self.sem[key], 16)
        self.n_ins += 1
        self._mark([idx], [out], (key, self.cnt[key]))

    def barrier(self):
        need = {k: c for k, c in self.cnt.items() if c > 0}
        for e in self.eng:
            self._wait(e, dict(need))

    def finish(self):
        need = {k: c for k, c in self.cnt.items() if c > 0}
        self._wait("sp", need)


def build(stage=99, NPOOL=2560):
    import os
    nc = bass.Bass("TRN2", target_bir_lowering=False)
    es = ExitStack()
    C = Ctx(nc, es)
    DBG = stage != 99

    def dram_in(name, shape, dt=F32):
        return nc.dram_tensor(name, list(shape), dt, kind="ExternalInput").ap()

    def dram_out(name, shape, dt=F32):
        return nc.dram_tensor(name, list(shape), dt, kind="ExternalOutput").ap()

    def sb(name, shape, dt=F32, parts=1):
        return Buf(es.enter_context(nc.sbuf_tensor(name, list(shape), dt)), parts)

    def ps(name, shape, dt=F32, parts=1):
        return Buf(es.enter_context(nc.psum_tensor(name, list(shape), dt)), parts)

    def A(out, in_, func, bias=None, scale=None, accum=None, eng="act"):
        kw = {}
        if bias is not None:
            kw["bias"] = bias.ap if isinstance(bias, V) else bias
        if scale is not None:
            kw["scale"] = scale.ap if isinstance(scale, V) else scale
        if accum is not None:
            kw["accum_out"] = accum.ap
        R = [in_] + [x for x in (bias, scale) if isinstance(x, V)]
        W = [out] + ([accum] if accum is not None else [])
        C.op("act", lambda: nc.scalar.activation(out=out.ap, in_=in_.ap, func=func, **kw), R=R, W=W)

    def CPY(out, in_, eng="act"):
        if eng == "act":
            C.op("act", lambda: nc.scalar.copy(out=out.ap, in_=in_.ap), R=[in_], W=[out])
        else:
            e = nc.vector if eng == "dve" else nc.gpsimd
            C.op(eng, lambda: e.tensor_copy(out=out.ap, in_=in_.ap), R=[in_], W=[out])

    def TT(out, a, b, op, eng="dve"):
        e = nc.vector if eng == "dve" else nc.gpsimd
        C.op(eng, lambda: e.tensor_tensor(out=out.ap, in0=a.ap, in1=b.ap, op=op), R=[a, b], W=[out])

    def TS(out, a, s1, s2, op0, op1=None, eng="dve"):
        e = nc.vector if eng == "dve" else nc.gpsimd
        R = [a] + [x for x in (s1, s2) if isinstance(x, V)]
        v1 = s1.ap if isinstance(s1, V) else s1
        v2 = s2.ap if isinstance(s2, V) else s2
        if op1 is None:
            C.op(eng, lambda: e.tensor_scalar(out=out.ap, in0=a.ap, scalar1=v1, scalar2=None, op0=op0), R=R, W=[out])
        else:
            C.op(eng, lambda: e.tensor_scalar(out=out.ap, in0=a.ap, scalar1=v1, scalar2=v2, op0=op0, op1=op1), R=R, W=[out])

    def STT(out, in0, scalar, in1, op0, op1, eng="dve"):
        e = nc.vector if eng == "dve" else nc.gpsimd
        R = [in0, in1] + ([scalar] if isinstance(scalar, V) else [])
        sv = scalar.ap if isinstance(scalar, V) else scalar
        C.op(eng, lambda: e.scalar_tensor_tensor(out=out.ap, in0=in0.ap, scalar=sv, in1=in1.ap, op0=op0, op1=op1), R=R, W=[out])

    def MM(out, lhsT, rhs, start=True, stop=True):
        C.op("pe", lambda: nc.tensor.matmul(out.ap, lhsT.ap, rhs.ap, start=start, stop=stop),
             R=[lhsT, rhs], W=[out], inc=stop)

    def MUL(out, in_, c):
        C.op("act", lambda: nc.scalar.mul(out=out.ap, in_=in_.ap, mul=c), R=[in_], W=[out])

    def MEMSET(out, val, eng="pool"):
        e = nc.vector if eng == "dve" else nc.gpsimd
        C.op(eng, lambda: e.memset(out.ap, val), W=[out])

    x_all = dram_in("x_all", [NTT * 128, D])
    c5T = dram_in("c5T", [128, 8, 5])
    w_ada = dram_in("w_ada", [DEPTH, D, 6 * D])
    b_ada = dram_in("b_ada", [DEPTH, 6 * D])
    norm_g = dram_in("norm_g", [DEPTH, 4, D])
    w_in = dram_in("w_in", [DEPTH, D, NIN])
    b_fox = dram_in("b_fox", [DEPTH, 6])
    w_out = dram_in("w_out", [DEPTH, D, D])
    w_pool = dram_in("w_pool", [DEPTH, 4, 64, 64])
    pool_scale = dram_in("pool_scale", [DEPTH, 256])
    w_ff1 = dram_in("w_ff1", [DEPTH, D, DFF])
    w_ff2 = dram_in("w_ff2", [DEPTH, DFF, D])
    cst = dram_in("cst", [128, CST_N])
    cstb = dram_in("cstb", [128, CSTB_N])
    win_in = dram_in("win_in", [DEPTH, NS, 512, 256])
    pool_in = dram_in("pool_in", [DEPTH, NS, 15, 256])
    c_fox = [dram_in("c_fox%d" % l_, [NPOOL * 128, 768]) for l_ in range(DEPTH)]
    c_logf = [dram_in("c_logf%d" % l_, [NPOOL * 128, 6]) for l_ in range(DEPTH)]
    c_nsa = [dram_in("c_nsa%d" % l_, [NPOOL * 128, 512]) for l_ in range(DEPTH)]
    ptab = dram_in("ptab", [1, NS * NPAGE], I32)

    y_dram = dram_out("y_all", [NTT * 128, D])
    o_foxkv = dram_out("o_foxkv", [DEPTH, NTT * 128, 768])
    o_logf = dram_out("o_logf", [DEPTH, 128, NTT, 6])
    o_nsakv = dram_out("o_nsakv", [DEPTH, NTT * 128, 512])
    o_win = dram_out("o_win", [DEPTH, 5 * 128, 256])
    o_u = dram_out("o_u", [DEPTH, 2 * 128, 256])
    o_win_s = dram_out("o_win_s", [DEPTH, NS, 512, 256])
    o_pool_s = dram_out("o_pool_s", [DEPTH, NS, 15, 256])
    modscr = nc.dram_tensor("modscr", [5, 6 * D], F32, kind="Internal").ap()
    modscr_b = Buf(modscr)
    y_all = Buf(y_dram, NTT)
    if DBG:
        dbg_ocat = dram_out("dbg_ocat", [NTT * 128, D], BF16)

    CST = sb("CST", [128, CST_N])
    C.dma("sp", CST[:, :], cst[:, :])
    CB = sb("CB", [128, CSTB_N], BF16)
    C.dma("pool", CB[:, :], cstb[:, :])

    def cs(name, n):
        o = CO[name]
        return CST[:, o:o + n]

    def cb(name, n, rows=128):
        o = COB[name]
        return CB[0:rows, o:o + n]

    ident_b = cb("ident", 128)
    tri_le = cb("tri_le", 128)
    tri_gt = cb("tri_gt", 128)
    ones_b = sb("ones_b", [128, 128], BF16)
    MEMSET(ones_b[:, :], 1.0)

    wst = [sb("wst%d" % i, [128, 8, 512], BF16) for i in range(2)]
    wst_i = [0]

    def next_wst():
        b = wst[wst_i[0] % 2]
        wst_i[0] += 1
        return b

    pz = [ps("pz%d" % i, [128, 512]) for i in range(2)]
    pz_i = [0]

    def next_pz():
        b = pz[pz_i[0] % 2]
        pz_i[0] += 1
        return b

    pS = [ps("pS%d" % i, [128, 512]) for i in range(2)]
    pS_i = [0]

    def next_pS():
        b = pS[pS_i[0] % 2]
        pS_i[0] += 1
        return b

    ptr = ps("ptr", [128, 8, 128], BF16)
    pO = ps("pO", [128, 512])
    pM = ps("pM", [128, 512])
    PT = [sb("PT%d" % i, [128, 512], BF16) for i in range(3)]
    PT_i = [0]

    def next_PT():
        b = PT[PT_i[0] % 3]
        PT_i[0] += 1
        return b

    c5 = sb("c5", [128, 8, 5])
    C.dma("sp", c5[:, :, :], c5T[:, :, :])
    sc5 = sb("sc5", [128, 8, 5], BF16)
    A(sc5[:, :, :], c5[:, :, :], AF.Silu)
    mch = sb("mch", [5, 512])
    bch = sb("bch", [5, 512])

    def compute_mod(l):
        for cc in range(12):
            sl = slice(cc * 512, (cc + 1) * 512)
            C.dma("sp", bch[:, :], b_ada[l:l + 1, sl].to_broadcast([5, 512]))
            wb = next_wst()
            C.dma("pool", wb[:, :, :], w_ada[l, :, sl].rearrange("(k p) n -> p k n", p=128))
            pzb = next_pz()
            for kc in range(8):
                MM(pzb[0:5, :], sc5[:, kc, :], wb[:, kc, :], start=(kc == 0), stop=(kc == 7))
            CPY(mch[:, :], pzb[0:5, :])
            TT(mch[:, :], mch[:, :], bch[:, :], ALU.add)
            C.dma("sp", modscr_b[:, sl], mch[:, :])

    SCG = [[sb("SCG%d%d" % (w, i), [128, D], BF16) for i in range(2)] for w in range(2)]
    SHF = [[sb("SHF%d%d" % (w, i), [128, D], BF16) for i in range(2)] for w in range(2)]
    GG = [[sb("GG%d%d" % (w, i), [128, D], BF16) for i in range(2)] for w in range(2)]

    def load_mod(l, which):
        o = which * 3 * D
        for i in range(2):
            if i == 0:
                def src(j):
                    return V(modscr[0:1, o + j * D:o + (j + 1) * D].to_broadcast([128, D]), modscr_b.trks)
                rows = slice(0, 128)
            else:
                def src(j):
                    return V(modscr[1:5, o + j * D:o + (j + 1) * D], modscr_b.trks)
                rows = slice(0, 4)
                for b_ in (SCG[which][1], SHF[which][1], GG[which][1]):
                    MEMSET(b_[:, :], 0.0)
            C.dma("pool", SHF[which][i][rows, :], src(0))
            C.dma("pool", SCG[which][i][rows, :], src(1))
            C.dma("pool", GG[which][i][rows, :], src(2))
            C.dma("sp", sq[:, :], norm_g[l, 2 * which:2 * which + 1, :].to_broadcast([128, D]))
            STT(SCG[which][i][:, :], SCG[which][i][:, :], 1.0, sq[:, :], ALU.add, ALU.mult)
            C.dma("sp", sq[:, :], norm_g[l, 2 * which + 1:2 * which + 2, :].to_broadcast([128, D]))
            TT(GG[which][i][:, :], GG[which][i][:, :], sq[:, :], ALU.mult)

    xt = sb("xt", [128, D])
    hb = sb("hb", [128, D], BF16)
    sq = sb("sq", [128, D])
    stat = sb("stat", [128, 4])
    hT4 = sb("hT4", [128, 8, 512], BF16)

    def rstd_of(xv, out_col):
        A(sq[:, :], xv, AF.Square, scale=float(D ** -0.5), accum=stat[:, 0:1])
        TS(stat[:, 1:2], stat[:, 0:1], EPS, None, ALU.add)
        A(stat[:, 1:2], stat[:, 1:2], AF.Sqrt)
        C.op("dve", lambda: nc.vector.reciprocal(out=stat.t[:, out_col:out_col + 1], in_=stat.t[:, 1:2]),
             R=[stat[:, :]], W=[stat[:, :]])

    def transpose8(src_bf, dst):
        for kc in range(8):
            C.op("pe", lambda kc=kc: nc.tensor.transpose(ptr.t[:, kc, :], src_bf.ap[:, kc * 128:(kc + 1) * 128], ident_b.ap),
                 R=[src_bf, ident_b], W=[ptr[:, :, :]], inc=(kc == 7))
        CPY(dst, ptr[:, :, :])

    def norm_mod_T(src, t, j, which):
        mi = 1 if t == NT else 0
        C.dma("sp", xt[:, :], src)
        rstd_of(xt[:, :], 2)
        STT(sq[:, :], xt[:, :], stat[:, 2:3], SCG[which][mi][:, :], ALU.mult, ALU.mult)
        TT(hb[:, :], sq[:, :], SHF[which][mi][:, :], ALU.add)
        transpose8(hb[:, :], hT4[:, :, j * 128:(j + 1) * 128])

    arena1 = es.enter_context(nc.sbuf_tensor("arena1", [128, 16480], BF16))
    kT_fox = Buf(arena1[:, 0:6144].rearrange("p (a n) -> p a n", a=3))
    V_fox = Buf(arena1[:, 6144:12384].rearrange("p (t h d) -> p t h d", t=16, h=6))
    ksT2 = Buf(arena1[:, 12384:16480].rearrange("p (a n) -> p a n", a=2))
    A2 = Buf(arena1[:, 0:16384].rearrange("p (f n) -> p f n", f=32))
    arena2 = es.enter_context(nc.sbuf_tensor("arena2", [128, 4096], F32))
    YAC = Buf(arena2[:, :].rearrange("p (a n) -> p a n", a=4))
    kwT2 = sb("kwT2", [128, 2, T], BF16)
    V_s = sb("V_s", [128, NT, 2, 65], BF16)
    V_w = sb("V_w", [128, NT, 2, 65], BF16)
    kcmT2 = sb("kcmT2", [128, 2, 32], BF16)
    vcm = sb("vcm", [32, 2, 65], BF16)
    qT_fox = sb("qT_fox", [128, 3, 512], BF16)
    qT_nsa = sb("qT_nsa", [128, 3, 512], BF16)
    BselT = sb("BselT", [32, 2, 512], BF16)
    OC = Buf(arena2[:, 0:2048].bitcast(BF16).rearrange("p (a n) -> p a n", a=4))
    OCT = Buf(arena2[:, 2048:4096].bitcast(BF16).rearrange("p (a n) -> p a n", a=8))
    for vb in (V_fox, V_s, V_w):
        MEMSET(vb[:, :, :, 64:65], 1.0)
    MEMSET(vcm[:, :, 64:65], 1.0)

    bfox_t = sb("bfox_t", [128, 6])
    st_fox = sb("st_fox", [128, 768])
    st_nsa = sb("st_nsa", [128, 512])
    st_win = sb("st_win", [128, 256])
    st_u = sb("st_u", [128, 256])
    LOGF = sb("LOGF", [128, NTT, 6])
    CFULL = sb("CFULL", [128, NTT, 6])
    CI = sb("CI", [128, NTT, 6])
    BIASQ = sb("BIASQ", [128, NT, 24])
    GATES = sb("GATES", [128, NTT, 18])
    qn = sb("qn", [128, 384])
    rtmp = sb("rtmp", [128, 4, 6, 8])
    tb = sb("tb", [128, 512], BF16)
    Ub = [sb("Ub%d" % i, [128, 256], BF16) for i in range(2)]
    ls = [sb("ls%d" % i, [128, 24]) for i in range(4)]
    osb = sb("osb", [128, 4, 65])
    rec = sb("rec", [128, 8])
    onsa = sb("onsa", [128, 4, 384])
    vcmf = sb("vcmf", [32, 128])
    wp_sb = sb("wp_sb", [64, 4, 64], BF16)
    psc = sb("psc", [128, 256])
    dT = sb("dT", [64, 4, 128], BF16)
    impT = sb("impT", [32, 2, 512])
    ecmp = sb("ecmp", [32, 512])
    lcmp = sb("lcmp", [32, 512])
    imp_tm = sb("imp_tm", [128, 2, 32])
    top8 = sb("top8", [128, 8])
    selb = sb("selb", [128, 2, 32], BF16)
    ident_f = sb("ident_f", [32, 32])
    CPY(ident_f[:, :], cb("ident", 32, rows=32), eng="dve")

    def rope(v, nh, t):
        cos = cs("cos", NTT * 48).re("p (t h e) -> p t h e", t=NTT, h=6)[:, t, 0:nh, :]
        sin = cs("sin", NTT * 48).re("p (t h e) -> p t h e", t=NTT, h=6)[:, t, 0:nh, :]
        x1 = v[:, :, 0:8]
        x2 = v[:, :, 8:16]
        tm = [rtmp[:, i, 0:nh, :] for i in range(4)]
        TT(tm[0], x1, cos, ALU.mult)
        TT(tm[1], x2, sin, ALU.mult)
        TT(tm[2], x2, cos, ALU.mult)
        TT(tm[3], x1, sin, ALU.mult)
        TT(x1, tm[0], tm[1], ALU.subtract)
        TT(x2, tm[2], tm[3], ALU.add)

    def logsigmoid(L, n):
        a_, u_, y_, e_ = [b[:, 0:n] for b in ls]
        TS(a_, L, -1.0, None, ALU.mult)
        TT(a_, a_, L, ALU.max)
        A(u_, a_, AF.Exp, scale=-1.0)
        TS(a_, u_, 1.0, None, ALU.add)
        CPY(y_, u_, eng="dve")
        for it in range(5):
            A(e_, y_, AF.Exp, scale=-1.0)
            TT(e_, e_, a_, ALU.mult)
            STT(y_, y_, -1.0, e_, ALU.add, ALU.add)
        STT(L, L, 0.0, y_, ALU.min, ALU.subtract)

    def tr_into(dst, src_bf, nblk):
        for i in range(nblk):
            C.op("pe", lambda i=i: nc.tensor.transpose(ptr.t[:, i, :], src_bf.ap[:, i * 128:(i + 1) * 128], ident_b.ap),
                 R=[src_bf, ident_b], W=[ptr[:, :, :]], inc=(i == nblk - 1))
        CPY(dst, ptr[:, 0:nblk, :])

    def post(l, g, t, j, pzb):
        r0 = t * 128
        prompt = t < NT
        tc_ = slice(t * 128, (t + 1) * 128)
        jc = slice(j * 128, (j + 1) * 128)
        if g == 0:
            MUL(tb[:, 0:384], pzb[:, 0:384], SCALE)
            if prompt:
                tr_into(qT_fox[:, :, jc], tb[:, 0:384], 3)
            else:
                tr_into(qTs_fox[:, :, :], tb[:, 0:384], 3)
        elif g == 1:
            CPY(st_fox[:, 0:384], pzb[:, 0:384])
            C.dma("sp", o_foxkv[l, r0:r0 + 128, 0:384], st_fox[:, 0:384])
            CPY(tb[:, 0:384], st_fox[:, 0:384], eng="dve")
            if prompt:
                tr_into(kT_fox[:, :, tc_], tb[:, 0:384], 3)
            else:
                tr_into(kTs_new[:, :, :], tb[:, 0:384], 3)
        elif g == 2:
            CPY(st_fox[:, 384:768], pzb[:, 0:384])
            CPY(LOGF[:, t, :], pzb[:, 384:390])
            C.dma("sp", o_foxkv[l, r0:r0 + 128, 384:768], st_fox[:, 384:768])
            if prompt:
                CPY(V_fox[:, t, :, 0:64], st_fox[:, 384:768].re("p (h d) -> p h d", h=6), eng="dve")
            else:
                CPY(VAnew[:, :, 0:64], st_fox[:, 384:768].re("p (h d) -> p h d", h=6), eng="dve")
            TT(LOGF[:, t, :], LOGF[:, t, :], bfox_t[:, :], ALU.add)
            logsigmoid(LOGF[:, t, :], 6)
            if prompt:
                CPY(tb[:, 0:6], LOGF[:, t, :], eng="dve")
                CPY(ls[0][:, 0:6], tb[:, 0:6], eng="dve")
                TT(ls[0][:, 0:6], LOGF[:, t, :], ls[0][:, 0:6], ALU.subtract)
                CPY(tb[:, 8:14], ls[0][:, 0:6], eng="dve")
                MM(pM[:, 0:6], tri_le, tb[:, 0:6], start=True, stop=False)
                MM(pM[:, 0:6], tri_le, tb[:, 8:14], start=False, stop=True)
                MM(pM[:, 8:14], ones_b[:, :], tb[:, 0:6], start=True, stop=False)
                MM(pM[:, 8:14], ones_b[:, :], tb[:, 8:14], start=False, stop=True)
                if t == 0:
                    CPY(CI[:, 0, :], pM[:, 8:14])
                else:
                    CPY(CI[:, t, :], pM[:, 8:14])
                    TT(CI[:, t, :], CI[:, t, :], CI[:, t - 1, :], ALU.add)
                CPY(CFULL[:, t, :], pM[:, 0:6])
                if t > 0:
                    TT(CFULL[:, t, :], CFULL[:, t, :], CI[:, t - 1, :], ALU.add)
        elif g == 3:
            if prompt:
                CPY(qn[:, :], pzb[:, 0:384])
                rope(qn[:, :].re("p (h d) -> p h d", h=6), 6, t)
                MUL(tb[:, 0:384], qn[:, :], SCALE)
                tr_into(qT_nsa[:, :, jc], tb[:, 0:384], 3)
            else:
                CPY(qn[:, :], pzb[:, 0:384])
                rope(qn[:, :].re("p (h d) -> p h d", h=6), 6, t)
                MUL(tb[:, 0:384], qn[:, :], SCALE)
                tr_into(qTs_nsa[:, :, :], tb[:, 0:384], 3)
        elif g == 4:
            CPY(st_nsa[:, :], pzb[:, 0:512])
            v5 = st_nsa[:, :].re("p (a b h d) -> p a b h d", a=2, b=2, h=2)
            rope(v5[:, 0, 0, :, :], 2, t)
            rope(v5[:, 1, 0, :, :], 2, t)
            C.dma("sp", o_nsakv[l, r0:r0 + 128, :], st_nsa[:, :])
            if not prompt:
                tb4 = tb[:, 0:256].re("p (k r d) -> p k r d", k=2, r=2)
                ksv = st_nsa[:, 256:384].re("p (k d) -> p k d", k=2)
                CPY(tb4[:, :, 0, :], ksv, eng="dve")
                CPY(tb4[:, :, 1, :], ksv, eng="dve")
                tr_into(ksT_new[:, :, :], tb[:, 0:256], 2)
                CPY(VSnew[:, :, 0:64], st_nsa[:, 384:512].re("p (k d) -> p k d", k=2), eng="dve")
            if prompt:
                tb4 = tb[:, 0:256].re("p (k r d) -> p k r d", k=2, r=2)
                ksv = st_nsa[:, 256:384].re("p (k d) -> p k d", k=2)
                CPY(tb4[:, :, 0, :], ksv, eng="dve")
                CPY(tb4[:, :, 1, :], ksv, eng="dve")
                tr_into(ksT2[:, :, tc_], tb[:, 0:256], 2)
                CPY(V_s[:, t, :, 0:64], st_nsa[:, 384:512].re("p (k d) -> p k d", k=2), eng="dve")
                CPY(tb4[:, :, 0, :], st_nsa[:, 0:128].re("p (k d) -> p k d", k=2), eng="dve")
                CPY(tb4[:, :, 1, :], st_nsa[:, 0:128].re("p (k d) -> p k d", k=2), eng="dve")
                avg_t = cb("avg", 16 * 32).re("p (t j) -> p t j", t=16)[:, t, 2 * t:2 * t + 2]
                for kv in range(2):
                    MM(pM[:, 16 + 2 * kv:18 + 2 * kv], tb[:, kv * 128:(kv + 1) * 128], avg_t)
                CPY(kcmT2[:, :, 2 * t:2 * t + 2], pM[:, 16:20].re("p (k j) -> p k j", k=2))
                CPY(tb[:, 256:384], st_nsa[:, 128:256], eng="dve")
                avg_full = cb("avg", 16 * 32).re("p (t j) -> p t j", t=16)[:, t, :]
                MM(pM[0:32, 32:160], avg_full, tb[:, 256:384])
                if t == 0:
                    CPY(vcmf[:, :], pM[0:32, 32:160])
                else:
                    CPY(sq[0:32, 512:640], pM[0:32, 32:160])
                    TT(vcmf[:, :], vcmf[:, :], sq[0:32, 512:640], ALU.add)
                CPY(vcm[:, :, 0:64], vcmf[:, :].re("p (k d) -> p k d", k=2), eng="dve")
        elif g == 5:
            CPY(st_win[:, :], pzb[:, 0:256])
            A(GATES[:, t, :], pzb[:, 256:274], AF.Sigmoid)
            rope(st_win[:, :].re("p (b h d) -> p b h d", b=2, h=2)[:, 0, :, :], 2, t)
            if t >= 12 and prompt:
                C.dma("sp", o_win[l, (t - 12) * 128:(t - 11) * 128, :], st_win[:, :])
            if t == NT:
                C.dma("sp", o_win[l, 512:640, :], st_win[:, :])
                C.dma("sp", o_win_s[l, :, 511, :], st_win[0:NS, :])
                C.dma("sp", o_win_s[l, :, 0:511, :], win_in[l, :, 1:512, :])
            tb4 = tb[:, 0:256].re("p (k r d) -> p k r d", k=2, r=2)
            kwv = st_win[:, 0:128].re("p (k d) -> p k d", k=2)
            CPY(tb4[:, :, 0, :], kwv, eng="dve")
            CPY(tb4[:, :, 1, :], kwv, eng="dve")
            if prompt:
                tr_into(kwT2[:, :, tc_], tb[:, 0:256], 2)
                CPY(V_w[:, t, :, 0:64], st_win[:, 128:256].re("p (k d) -> p k d", k=2), eng="dve")
            else:
                tr_into(kwT_new[:, :, :], tb[:, 0:256], 2)
                CPY(VWnew[:, :, 0:64], st_win[:, 128:256].re("p (k d) -> p k d", k=2), eng="dve")
        elif g == 6:
            CPY(st_u[:, :], pzb[:, 0:256])
            if t >= 15:
                C.dma("sp", o_u[l, (t - 15) * 128:(t - 14) * 128, :], st_u[:, :])
            if t == NT:
                C.dma("sp", o_pool_s[l, :, 14, :], st_u[0:NS, :])
                C.dma("sp", o_pool_s[l, :, 0:14, :], pool_in[l, :, 1:15, :])
            if prompt:
                CPY(Ub[t % 2][:, :], st_u[:, :], eng="dve")
                pool_tile(l, t, j)

    def pool_tile(l, t, j):
        pd = next_pS()
        for w in range(4):
            cname = "bcur0_%d" % w if t == 0 else "bcur_%d" % w
            MM(pd[0:64, w * 128:(w + 1) * 128], Ub[t % 2][:, w * 64:(w + 1) * 64], cb(cname, 128), start=True, stop=(t == 0))
            if t > 0:
                MM(pd[0:64, w * 128:(w + 1) * 128], Ub[(t - 1) % 2][:, w * 64:(w + 1) * 64], cb("bprev_%d" % w, 128),
                   start=False, stop=True)
        CPY(dT[:, :, :], pd[0:64, :].re("p (w t) -> p w t", w=4))
        po = next_pS()
        for w in range(4):
            MM(po[:, w * 64:(w + 1) * 64], dT[:, w, :], wp_sb[:, w, :])
        CPY(sq[:, 0:256], po[:, 0:256])
        TT(OC[:, j, 768:1024], sq[:, 0:256], psc[:, :], ALU.mult)

    def attend(steps, q_of, k_of, v_of, bias_of, extra_of=None):
        first = {}
        last = {}
        for si, (kt, c0, c1, mk) in enumerate(steps):
            for sub in range(c0 // 128, c1 // 128):
                first.setdefault(sub, si)
                last[sub] = si
        for si, (kt, c0, c1, mk) in enumerate(steps):
            pSb = next_pS()
            ex = extra_of(kt) if extra_of is not None else None
            MM(pSb[:, c0:c1], k_of(kt), q_of(c0, c1), start=True, stop=(ex is None))
            if ex is not None:
                MM(pSb[:, c0:c1], ex[0], ex[1][:, c0:c1], start=False, stop=True)
            P_ = next_PT()
            if bias_of is None:
                A(P_[:, c0:c1], pSb[:, c0:c1], AF.Exp)
            else:
                for sub in range(c0 // 128, c1 // 128):
                    A(P_[:, sub * 128:(sub + 1) * 128], pSb[:, sub * 128:(sub + 1) * 128], AF.Exp, bias=bias_of(kt, sub))
            if mk == "le":
                TT(P_[:, c0:c0 + 128], P_[:, c0:c0 + 128], tri_le, ALU.mult)
            elif mk == "gt":
                TT(P_[:, c1 - 128:c1], P_[:, c1 - 128:c1], tri_gt, ALU.mult)
            for sub in range(c0 // 128, c1 // 128):
                is_first = (si == 0 and sub == c0 // 128)
                C.op("pe", lambda sub=sub, is_first=is_first: nc.tensor.matmul(
                    pO.t[:, sub * 65:(sub + 1) * 65], P_.t[:, sub * 128:(sub + 1) * 128], v_of(kt).ap,
                    start=is_first, stop=(last[sub] == si), skip_group_check=True),
                    R=[P_[:, :], v_of(kt)], W=[pO[:, :]], inc=(last[sub] == si))

    def finish_head(dst_of_sub=None, gate_col=None, Q=None, first_branch=False, hslot=None):
        CPY(osb[:, :, :], pO[:, 0:260].re("p (s d) -> p s d", s=4))
        TS(rec[:, 0:4], osb[:, :, 64], 1e-30, None, ALU.max)
        C.op("dve", lambda: nc.vector.reciprocal(out=rec.t[:, 0:4], in_=rec.t[:, 0:4]), R=[rec[:, :]], W=[rec[:, :]])
        if gate_col is None:
            for sub in range(4):
                TS(dst_of_sub(sub), osb[:, sub, 0:64], rec[:, sub:sub + 1], None, ALU.mult)
        else:
            TT(rec[:, 4:8], rec[:, 0:4], GATES[:, 4 * Q:4 * Q + 4, gate_col], ALU.mult)
            for sub in range(4):
                dst = onsa[:, sub, hslot * 64:(hslot + 1) * 64]
                if first_branch:
                    TS(dst, osb[:, sub, 0:64], rec[:, 4 + sub:5 + sub], None, ALU.mult)
                else:
                    STT(dst, osb[:, sub, 0:64], rec[:, 4 + sub:5 + sub], dst, ALU.mult, ALU.add)

    def causal_steps(Q):
        st = []
        for kt in range(4 * Q + 4):
            if kt < 4 * Q:
                st.append((kt, 0, 512, None))
            else:
                st.append((kt, (kt - 4 * Q) * 128, 512, "le"))
        return st

    def window_steps(Q):
        st = []
        for kt in range(max(0, 4 * Q - 4), 4 * Q):
            i = kt - (4 * Q - 4)
            st.append((kt, 0, 128 * (i + 1), "gt"))
        for kt in range(4 * Q, 4 * Q + 4):
            st.append((kt, (kt - 4 * Q) * 128, 512, "le"))
        return st

    def attention_block(l, Q):
        tiles = list(range(4 * Q, 4 * Q + 4))
        for kt in range(4 * Q + 4):
            for sub in range(4):
                TT(BIASQ[:, kt, sub * 6:(sub + 1) * 6], CI[:, 4 * Q + sub, :], CFULL[:, kt, :], ALU.subtract)
        for h in range(6):
            pr, bs = h // 2, 64 * (h % 2)
            attend(causal_steps(Q),
                   q_of=lambda c0, c1: qT_fox[bs:bs + 64, pr, c0:c1],
                   k_of=lambda kt: kT_fox[bs:bs + 64, pr, kt * 128:(kt + 1) * 128],
                   v_of=lambda kt: V_fox[:, kt, h, :],
                   bias_of=lambda kt, sub: BIASQ[:, kt, sub * 6 + h:sub * 6 + h + 1])
            finish_head(dst_of_sub=lambda sub: OC[:, sub, h * 64:(h + 1) * 64])
        nblk = 8 * Q + 8
        cmask = cb("cmpm", T, rows=32)[:, Q * 512:(Q + 1) * 512]
        for kv in range(2):
            for g in range(3):
                h = kv * 3 + g
                pr, bs = h // 2, 64 * (h % 2)
                pSb = next_pS()
                MM(pSb[0:32, :], kcmT2[bs:bs + 64, kv, :], qT_nsa[bs:bs + 64, pr, :])
                A(ecmp[:, :], pSb[0:32, :], AF.Exp)
                TT(ecmp[:, :], ecmp[:, :], cmask, ALU.mult)
                CPY(PT[0][0:32, :], ecmp[:, :], eng="dve")
                pL = next_pS()
                MM(pL[0:32, :], ones_b[0:32, 0:32], PT[0][0:32, :])
                TS(lcmp[:, :], pL[0:32, :], 1e-30, None, ALU.max)
                C.op("dve", lambda: nc.vector.reciprocal(out=lcmp.t[:, :], in_=lcmp.t[:, :]), R=[lcmp[:, :]], W=[lcmp[:, :]])
                if g == 0:
                    TT(impT[:, kv, :], ecmp[:, :], lcmp[:, :], ALU.mult)
                else:
                    TT(ecmp[:, :], ecmp[:, :], lcmp[:, :], ALU.mult)
                    TT(impT[:, kv, :], impT[:, kv, :], ecmp[:, :], ALU.add)
                for sub in range(4):
                    MM(pO[:, sub * 65:(sub + 1) * 65], PT[0][0:32, sub * 128:(sub + 1) * 128], vcm[:, kv, :])
                finish_head(gate_col=h * 3 + 0, Q=Q, first_branch=True, hslot=h)
        for j, t in enumerate(tiles):
            for kv in range(2):
                C.op("pe", lambda kv=kv, j=j: nc.tensor.transpose(pM.t[:, 256 + kv * 32:256 + kv * 32 + 32],
                                                                 impT.t[:, kv, j * 128:(j + 1) * 128], ident_f.t[:, :]),
                     R=[impT[:, :, :], ident_f[:, :]], W=[pM[:, :]])
            CPY(imp_tm[:, :, :], pM[:, 256:320].re("p (k j) -> p k j", k=2))
            sa = cs("selA", 512).re("p (t j) -> p t j", t=16)[:, t, :]
            sbv = cs("selB", 512).re("p (t j) -> p t j", t=16)[:, t, :]
            for kv in range(2):
                TT(imp_tm[:, kv, :], imp_tm[:, kv, :], sa, ALU.mult)
                TT(imp_tm[:, kv, :], imp_tm[:, kv, :], sbv, ALU.add)
                C.op("dve", lambda kv=kv: nc.vector.max(out=top8.t[:, :], in_=imp_tm.t[:, kv, :]),
                     R=[imp_tm[:, :, :]], W=[top8[:, :]])
                TS(imp_tm[:, kv, :], imp_tm[:, kv, :], top8[:, 7:8], None, ALU.is_ge)
                TS(selb[:, kv, :], imp_tm[:, kv, :], -1.0, -NEGB, ALU.add, ALU.mult)
            for kv in range(2):
                C.op("pe", lambda kv=kv: nc.tensor.transpose(ptr.t[0:32, kv, :], selb.t[:, kv, :], ident_b.ap),
                     R=[selb[:, :, :], ident_b], W=[ptr[:, :, :]], inc=(kv == 1))
            CPY(BselT[:, :, j * 128:(j + 1) * 128], ptr[0:32, 0:2, :])
        for h in range(6):
            kv = h // 3
            pr, bs = h // 2, 64 * (h % 2)
            attend(causal_steps(Q),
                   q_of=lambda c0, c1: qT_nsa[bs:bs + 64, pr, c0:c1],
                   k_of=lambda kt: ksT2[bs:bs + 64, kv, kt * 128:(kt + 1) * 128],
                   v_of=lambda kt: V_s[:, kt, kv, :],
                   bias_of=None,
                   extra_of=lambda kt: (cb("expand", 16 * 128, rows=32).re("p (t k) -> p t k", t=16)[:, kt, :], BselT[:, kv, :]))
            finish_head(gate_col=h * 3 + 1, Q=Q, first_branch=False, hslot=h)
            attend(window_steps(Q),
                   q_of=lambda c0, c1: qT_nsa[bs:bs + 64, pr, c0:c1],
                   k_of=lambda kt: kwT2[bs:bs + 64, kv, kt * 128:(kt + 1) * 128],
                   v_of=lambda kt: V_w[:, kt, kv, :],
                   bias_of=None)
            finish_head(gate_col=h * 3 + 2, Q=Q, first_branch=False, hslot=h)
        CPY(OC[:, :, 384:768], onsa[:, :, :], eng="dve")

    def aview(o, n, pat=None, **kw):
        ap = arena1[:, o:o + n]
        if pat is not None:
            ap = ap.rearrange(pat, **kw)
        return Buf(ap)

    FKV = [aview(0, 768), aview(768, 768)]
    VA = [aview(1536, 390, "p (h d) -> p h d", h=6), aview(1926, 390, "p (h d) -> p h d", h=6)]
    kTp = [aview(2316, 384, "p (a n) -> p a n", a=3), aview(2700, 384, "p (a n) -> p a n", a=3)]
    PB = [aview(3084, 24, "p (h b) -> p h b", h=6), aview(3108, 24, "p (h b) -> p h b", h=6)]
    NKV = [aview(3132, 512), aview(3644, 512)]
    KD = [aview(4156, 256), aview(4412, 256)]
    kT2p = [aview(4668, 256, "p (a n) -> p a n", a=2), aview(4924, 256, "p (a n) -> p a n", a=2)]
    VSA = [aview(5180, 130, "p (k d) -> p k d", k=2), aview(5310, 130, "p (k d) -> p k d", k=2)]
    kcmT_s = aview(5440, 256, "p (a n) -> p a n", a=2)
    vcmA_s = aview(5696, 130, "p (k d) -> p k d", k=2)
    qTs_fox = aview(5826, 384, "p (a n) -> p a n", a=3)
    qTs_nsa = aview(6210, 384, "p (a n) -> p a n", a=3)
    kTs_new = aview(6594, 384, "p (a n) -> p a n", a=3)
    VAnew = aview(6978, 390, "p (h d) -> p h d", h=6)
    ksT_new = aview(7368, 256, "p (a n) -> p a n", a=2)
    kwT_new = aview(7624, 256, "p (a n) -> p a n", a=2)
    VSnew = aview(7880, 130, "p (k d) -> p k d", k=2)
    VWnew = aview(8010, 130, "p (k d) -> p k d", k=2)
    WKV = aview(8140, 1024, "p (t c) -> p t c", t=4)
    HI = aview(9200, 384)
    LO = aview(9584, 384)
    PBc = aview(9968, 24, "p (h b) -> p h b", h=6)
    SBrow = aview(9992, 128)
    e6b = aview(10120, 8)
    dbf = aview(10128, 256)

    def a2view(o, n, pat=None, **kw):
        ap = arena2[:, o:o + n]
        if pat is not None:
            ap = ap.rearrange(pat, **kw)
        return Buf(ap)

    LFp = a2view(512, 384, "p (g h) -> p g h", h=6)
    Wp = a2view(896, 384, "p (g h) -> p g h", h=6)
    INC = [a2view(1280, 384, "p (g h) -> p g h", h=6), a2view(1664, 384, "p (g h) -> p g h", h=6)]
    BIASD = sb("BIASD", [128, NPAGE, 6])
    MASK6 = sb("MASK6", [128, NPAGE, 6])
    MASKB = sb("MASKB", [128, NPAGE, 2])
    ACCw = sb("ACCw", [128, 390])
    onsa_flat = onsa[:, :, :].re("p a n -> p (a n)")
    ACC = {"fox": onsa_flat[:, 0:390], "cmp": onsa_flat[:, 390:780], "sel": onsa_flat[:, 780:1170], "win": ACCw[:, :]}
    IDX = sb("IDX", [128, NS * NPAGE], I32)
    e6 = sb("e6", [128, 8])
    e6f = sb("e6f", [128, 8])
    imp2 = sb("imp2", [128, 2])
    SC = sb("SC", [2, 136])
    t4 = sq
    dsm = st_win
    C.dma("sp", IDX[:, :], ptab[0:1, :].to_broadcast([128, NS * NPAGE]))
    CPY(sq[:, 0:NS * NPAGE], IDX[:, :], eng="dve")
    TS(sq[:, 0:NS * NPAGE], sq[:, 0:NS * NPAGE], 128.0, cs("pidx", 1), ALU.mult, ALU.add)
    CPY(IDX[:, :], sq[:, 0:NS * NPAGE], eng="dve")

    def acc_into(name, b):
        CPY(t4[0:4, 0:390], pO[0:4, 0:390])
        if b == 0:
            CPY(ACC[name][0:4, :], t4[0:4, 0:390], eng="dve")
        else:
            TT(ACC[name][0:4, :], ACC[name][0:4, :], t4[0:4, 0:390], ALU.add)

    dec_first = [True]

    def dec_step(i, kT_of_h, v_of_h, q_of_h, b, bias_tile=None, bias_col=None, is_first=False, is_last=False):
        pSd = next_pS()
        for h in range(6):
            MM(pSd[:, h:h + 1], kT_of_h(h), q_of_h(h))
        CPY(e6[:, 0:6], pSd[:, 0:6])
        if bias_tile is not None:
            TT(e6[:, 0:6], e6[:, 0:6], bias_tile, ALU.add)
        if bias_col is not None:
            TS(e6[:, 0:6], e6[:, 0:6], bias_col, None, ALU.add)
        A(PB[i][:, :, b], e6[:, 0:6], AF.Exp)
        for h in range(6):
            first = is_first and h == 0
            last = is_last
            C.op("pe", lambda h=h, first=first, last=last: nc.tensor.matmul(
                pO.t[0:4, h * 65:(h + 1) * 65], PB[i].t[:, h, :], v_of_h(h).ap,
                start=first, stop=last, skip_group_check=True),
                R=[PB[i][:, :, :], v_of_h(h)], W=[pO[:, :]], inc=last)

    def hp(h):
        return h // 2, 64 * (h % 2)

    def decode_block(l):
        frows = c_fox[l][:, :]
        lrows = c_logf[l][:, :]
        nrows = c_nsa[l][:, :]
        for vb in (VA[0], VA[1], VAnew):
            MEMSET(vb[:, :, 64:65], 1.0)
        for vb in (VSA[0], VSA[1], VSnew, VWnew, vcmA_s):
            MEMSET(vb[:, :, 64:65], 1.0)
        for b in range(NS):
            for i in range(2):
                MEMSET(PB[i][:, :, :], 0.0)
            MEMSET(PBc[:, :, :], 0.0)
            nb_col = cs("negrow", 4)[:, b:b + 1]
            for pg in range(NPAGE):
                C.idma(LFp[:, pg, :], lrows, IDX[:, b * NPAGE + pg:b * NPAGE + pg + 1])
            Lf = LFp[:, :, :].re("p g h -> p (g h)")
            CPY(HI[:, :], Lf, eng="dve")
            CPY(sq[:, 0:384], HI[:, :], eng="dve")
            TT(sq[:, 0:384], Lf, sq[:, 0:384], ALU.subtract)
            CPY(LO[:, :], sq[:, 0:384], eng="dve")
            MM(pM[:, 0:384], tri_le, HI[:, :], start=True, stop=False)
            MM(pM[:, 0:384], tri_le, LO[:, :], start=False, stop=True)
            CPY(Wp[:, :, :].re("p g h -> p (g h)"), pM[:, 0:384])
            pzb = next_pz()
            MM(pzb[:, 0:384], ones_b[:, :], HI[:, :], start=True, stop=False)
            MM(pzb[:, 0:384], ones_b[:, :], LO[:, :], start=False, stop=True)
            CPY(INC[0][:, :, :].re("p g h -> p (g h)"), pzb[:, 0:384])
            cur = 0
            d = 1
            while d < NPAGE:
                nxt = 1 - cur
                CPY(INC[nxt][:, :, :], INC[cur][:, :, :], eng="dve")
                TT(INC[nxt][:, 0:NPAGE - d, :], INC[cur][:, 0:NPAGE - d, :], INC[cur][:, d:NPAGE, :], ALU.add)
                cur = nxt
                d *= 2
            TT(BIASD[:, :, :], INC[cur][:, :, :], Wp[:, :, :], ALU.subtract)
            TS(e6f[:, 0:6], LOGF[:, NT, :], -1.0, nb_col, ALU.mult, ALU.add)
            for pg in range(NPAGE):
                i = pg % 2
                C.idma(FKV[i][:, :], frows, IDX[:, b * NPAGE + pg:b * NPAGE + pg + 1])
                CPY(VA[i][:, :, 0:64], FKV[i][:, 384:768].re("p (h d) -> p h d", h=6), eng="dve")
                tr_into(kTp[i][:, :, :], FKV[i][:, 0:384], 3)
                dec_step(i, lambda h: kTp[i][hp(h)[1]:hp(h)[1] + 64, hp(h)[0], :],
                         lambda h: VA[i][:, h, :],
                         lambda h: qTs_fox[hp(h)[1]:hp(h)[1] + 64, hp(h)[0], b:b + 1],
                         b, bias_tile=BIASD[:, pg, :], is_first=(pg == 0))
            dec_step(0, lambda h: kTs_new[hp(h)[1]:hp(h)[1] + 64, hp(h)[0], :],
                     lambda h: VAnew[:, h, :],
                     lambda h: qTs_fox[hp(h)[1]:hp(h)[1] + 64, hp(h)[0], b:b + 1],
                     b, bias_tile=e6f[:, 0:6], is_last=True)
            acc_into("fox", b)
            pvc = pz[1]
            for pg in range(NPAGE):
                i = pg % 2
                C.idma(NKV[i][:, :], nrows, IDX[:, b * NPAGE + pg:b * NPAGE + pg + 1])
                kd4 = KD[i][:, :].re("p (k r d) -> p k r d", k=2, r=2)
                kcv = NKV[i][:, 0:128].re("p (k d) -> p k d", k=2)
                CPY(kd4[:, :, 0, :], kcv, eng="dve")
                CPY(kd4[:, :, 1, :], kcv, eng="dve")
                avg2 = cb("avg", 16 * 32).re("p (t j) -> p t j", t=16)[:, 0, 0:2]
                for kv in range(2):
                    MM(pM[:, 400 + 2 * kv:402 + 2 * kv], KD[i][:, kv * 128:(kv + 1) * 128], avg2)
                CPY(kcmT_s[:, :, 2 * pg:2 * pg + 2], pM[:, 400:404].re("p (k j) -> p k j", k=2))
                aw = cb("avgwin", 254)[:, 126 - 2 * pg:254 - 2 * pg]
                MM(pvc[:, 0:128], aw, NKV[i][:, 128:256], start=(pg == 0), stop=(pg == NPAGE - 1))
            CPY(vcmA_s[:, :, 0:64], pvc[:, 0:128].re("p (k d) -> p k d", k=2))
            pSd = next_pS()
            for h in range(6):
                MM(pSd[:, h:h + 1], kcmT_s[hp(h)[1]:hp(h)[1] + 64, h // 3, :], qTs_nsa[hp(h)[1]:hp(h)[1] + 64, hp(h)[0], b:b + 1])
            A(e6f[:, 0:6], pSd[:, 0:6], AF.Exp)
            CPY(e6b[:, 0:6], e6f[:, 0:6], eng="dve")
            pL = next_pS()
            MM(pL[:, 0:6], ones_b[:, :], e6b[:, 0:6])
            CPY(e6[:, 0:6], pL[:, 0:6])
            C.op("dve", lambda: nc.vector.reciprocal(out=e6.t[:, 0:6], in_=e6.t[:, 0:6]), R=[e6[:, :]], W=[e6[:, :]])
            TT(e6[:, 0:6], e6[:, 0:6], e6f[:, 0:6], ALU.mult)
            p3 = e6[:, 0:6].re("p (k g) -> p k g", k=2)
            TT(imp2[:, :], p3[:, :, 0], p3[:, :, 1], ALU.add)
            TT(imp2[:, :], imp2[:, :], p3[:, :, 2], ALU.add)
            CPY(PBc[:, :, b], e6b[:, 0:6], eng="dve")
            for h in range(6):
                C.op("pe", lambda h=h: nc.tensor.matmul(pO.t[0:4, h * 65:(h + 1) * 65], PBc.t[:, h, :], vcmA_s.t[:, h // 3, :],
                                                        start=(h == 0), stop=True, skip_group_check=True),
                     R=[PBc[:, :, :], vcmA_s[:, :, :]], W=[pO[:, :]])
            acc_into("cmp", b)
            C.op("pe", lambda: nc.tensor.transpose(pM.t[0:2, 0:128], imp2.t[:, :], cs("identf", 128).ap),
                 R=[imp2[:, :], cs("identf", 128)], W=[pM[:, :]])
            CPY(SC[:, 0:128], pM[0:2, 0:128])
            MEMSET(SC[:, 0:1], 1e4, eng="dve")
            MEMSET(SC[:, 127:128], 1e4, eng="dve")
            C.op("dve", lambda: nc.vector.max(out=top8.t[0:2, :], in_=SC.t[:, 0:128]), R=[SC[:, :]], W=[top8[:, :]])
            TS(SC[:, 0:128], SC[:, 0:128], top8[0:2, 6:7], None, ALU.is_ge)
            TS(SBrow[0:2, :], SC[:, 0:128], -1.0, -NEGB, ALU.add, ALU.mult)
            for kv in range(2):
                MM(pM[:, 128 + kv * 128:256 + kv * 128], cb("oneh", 256, rows=2)[:, kv * 128:(kv + 1) * 128], SBrow[0:2, :])
            for kv in range(2):
                pv = pM[:, 128 + kv * 128:256 + kv * 128].re("p (g two) -> p g two", two=2)
                CPY(MASKB[0:64, :, kv], pv[0:64, :, 0])
                CPY(MASKB[64:128, :, kv], pv[64:128, :, 1])
            for h in range(6):
                CPY(MASK6[:, :, h], MASKB[:, :, h // 3], eng="dve")
            for pg in range(NPAGE):
                i = pg % 2
                C.idma(NKV[i][:, :], nrows, IDX[:, b * NPAGE + pg:b * NPAGE + pg + 1])
                kd4 = KD[i][:, :].re("p (k r d) -> p k r d", k=2, r=2)
                ksv = NKV[i][:, 256:384].re("p (k d) -> p k d", k=2)
                CPY(kd4[:, :, 0, :], ksv, eng="dve")
                CPY(kd4[:, :, 1, :], ksv, eng="dve")
                tr_into(kT2p[i][:, :, :], KD[i][:, :], 2)
                CPY(VSA[i][:, :, 0:64], NKV[i][:, 384:512].re("p (k d) -> p k d", k=2), eng="dve")
                dec_step(i, lambda h: kT2p[i][hp(h)[1]:hp(h)[1] + 64, h // 3, :],
                         lambda h: VSA[i][:, h // 3, :],
                         lambda h: qTs_nsa[hp(h)[1]:hp(h)[1] + 64, hp(h)[0], b:b + 1],
                         b, bias_tile=MASK6[:, pg, :], is_first=(pg == 0))
            dec_step(0, lambda h: ksT_new[hp(h)[1]:hp(h)[1] + 64, h // 3, :],
                     lambda h: VSnew[:, h // 3, :],
                     lambda h: qTs_nsa[hp(h)[1]:hp(h)[1] + 64, hp(h)[0], b:b + 1],
                     b, bias_col=nb_col, is_last=True)
            acc_into("sel", b)
            C.dma("pool", WKV[:, :, :], win_in[l, b].rearrange("(t p) c -> p t c", p=128))
            for wt in range(4):
                i = wt % 2
                kd4 = KD[i][:, :].re("p (k r d) -> p k r d", k=2, r=2)
                kwv = WKV[:, wt, 0:128].re("p (k d) -> p k d", k=2)
                CPY(kd4[:, :, 0, :], kwv, eng="dve")
                CPY(kd4[:, :, 1, :], kwv, eng="dve")
                tr_into(kT2p[i][:, :, :], KD[i][:, :], 2)
                CPY(VSA[i][:, :, 0:64], WKV[:, wt, 128:256].re("p (k d) -> p k d", k=2), eng="dve")
                dec_step(i, lambda h: kT2p[i][hp(h)[1]:hp(h)[1] + 64, h // 3, :],
                         lambda h: VSA[i][:, h // 3, :],
                         lambda h: qTs_nsa[hp(h)[1]:hp(h)[1] + 64, hp(h)[0], b:b + 1],
                         b, bias_col=(cs("neg0", 1) if wt == 0 else None), is_first=(wt == 0))
            dec_step(0, lambda h: kwT_new[hp(h)[1]:hp(h)[1] + 64, h // 3, :],
                     lambda h: VWnew[:, h // 3, :],
                     lambda h: qTs_nsa[hp(h)[1]:hp(h)[1] + 64, hp(h)[0], b:b + 1],
                     b, bias_col=nb_col, is_last=True)
            acc_into("win", b)
        MEMSET(OC[:, 0, :], 0.0, eng="dve")
        for bi, name in enumerate(("fox", "cmp", "sel", "win")):
            a3 = ACC[name][0:4, :].re("p (h d) -> p h d", h=6)
            TS(rec[0:4, 0:6], a3[:, :, 64], 1e-30, None, ALU.max)
            C.op("dve", lambda: nc.vector.reciprocal(out=rec.t[0:4, 0:6], in_=rec.t[0:4, 0:6]), R=[rec[:, :]], W=[rec[:, :]])
            if name != "fox":
                gv = GATES[0:4, NT, :].re("p (h c) -> p h c", h=6)[:, :, bi - 1]
                TT(rec[0:4, 0:6], rec[0:4, 0:6], gv, ALU.mult)
            for h in range(6):
                if name == "fox":
                    TS(OC[0:4, 0, h * 64:(h + 1) * 64], a3[:, h, 0:64], rec[0:4, h:h + 1], None, ALU.mult)
                elif name == "cmp":
                    TS(qn[0:4, h * 64:(h + 1) * 64], a3[:, h, 0:64], rec[0:4, h:h + 1], None, ALU.mult)
                else:
                    STT(qn[0:4, h * 64:(h + 1) * 64], a3[:, h, 0:64], rec[0:4, h:h + 1], qn[0:4, h * 64:(h + 1) * 64],
                        ALU.mult, ALU.add)
        CPY(OC[0:4, 0, 384:768], qn[0:4, :], eng="dve")
        MEMSET(dsm[:, :], 0.0, eng="dve")
        for wi, w in enumerate((2, 4, 8, 16)):
            pst = obuf[:, 0:960].re("p (r c) -> p r c", c=64)
            C.dma("sp", pst[0:4, 0:w - 1, :], pool_in[l, :, 15 - (w - 1):15, wi * 64:(wi + 1) * 64])
            C.op("dve", lambda w=w: nc.vector.tensor_reduce(out=dsm.t[0:4, wi * 64:(wi + 1) * 64],
                                                           in_=pst.ap[0:4, 0:w - 1, :].rearrange("p r c -> p c r"),
                                                           axis=AX.X, op=ALU.add),
                 R=[pst], W=[dsm[:, :]])
            un = st_u[0:4, wi * 64:(wi + 1) * 64]
            dv = dsm[0:4, wi * 64:(wi + 1) * 64]
            TT(dv, dv, un, ALU.add)
            STT(dv, dv, 1.0 / w, un, ALU.mult, ALU.subtract)
        CPY(dbf[:, :], dsm[:, :], eng="dve")
        for g4 in range(4):
            C.op("pe", lambda g4=g4: nc.tensor.transpose(ptr.t[0:64, g4, :], dbf.t[:, g4 * 64:(g4 + 1) * 64], ident_b.ap),
                 R=[dbf[:, :], ident_b], W=[ptr[:, :, :]], inc=(g4 == 3))
        CPY(dT[:, :, :], ptr[0:64, 0:4, :])
        po = next_pS()
        for w in range(4):
            MM(po[:, w * 64:(w + 1) * 64], dT[:, w, :], wp_sb[:, w, :])
        CPY(sq[:, 0:256], po[:, 0:256])
        TT(OC[:, 0, 768:1024], sq[:, 0:256], psc[:, :], ALU.mult)

    obuf = sb("obuf", [128, D])

    def outproj_block(l, tiles, src_buf_of, dst_tile_of, which):
        nb = len(tiles)
        for j, t in enumerate(tiles):
            transpose8(OC[:, j, :], OCT[:, :, j * 128:(j + 1) * 128])
        wh = []
        for half in range(2):
            wb = next_wst()
            C.dma("pool", wb[:, :, :], w_out[l, :, half * 512:(half + 1) * 512].rearrange("(k p) n -> p k n", p=128))
            wh.append(wb)
        for j, t in enumerate(tiles):
            for half in range(2):
                pzb = next_pz()
                for kc in range(8):
                    MM(pzb[:, :], OCT[:, kc, j * 128:(j + 1) * 128], wh[half][:, kc, :], start=(kc == 0), stop=(kc == 7))
                CPY(obuf[:, half * 512:(half + 1) * 512], pzb[:, :])
            residual_update(l, t, which)

    def residual_update(l, t, which):
        mi = 1 if t == NT else 0
        rstd_of(obuf[:, :], 3)
        STT(obuf[:, :], obuf[:, :], stat[:, 3:4], GG[which][mi][:, :], ALU.mult, ALU.mult)
        if l == 0 and which == 0:
            C.dma("sp", xt[:, :], x_all[t * 128:(t + 1) * 128, :])
        else:
            C.dma("sp", xt[:, :], y_all.p(t)[t * 128:(t + 1) * 128, :])
        TT(obuf[:, :], obuf[:, :], xt[:, :], ALU.add)
        C.dma("sp", y_all.p(t)[t * 128:(t + 1) * 128, :], obuf[:, :])


    def mlp_block(l, tiles, a2):
        nb = len(tiles)
        n = nb * 128
        for j, t in enumerate(tiles):
            norm_mod_T(y_all.p(t)[t * 128:(t + 1) * 128, :], t, j, 1)
        for fc in range(8):
            wb = next_wst()
            C.dma("pool", wb[:, :, :], w_ff1[l, :, fc * 512:(fc + 1) * 512].rearrange("(k p) n -> p k n", p=128))
            for f4 in range(4):
                pzb = next_pz()
                for kc in range(8):
                    MM(pzb[:, 0:n], wb[:, kc, f4 * 128:(f4 + 1) * 128], hT4[:, kc, 0:n], start=(kc == 0), stop=(kc == 7))
                CPY(sq[:, 0:n], pzb[:, 0:n])
                TS(sq[:, 0:n], sq[:, 0:n], 0.0, None, ALU.max)
                TT(a2[:, fc * 4 + f4, 0:n], sq[:, 0:n], sq[:, 0:n], ALU.mult)
        yacc = [obuf]
        for j, t in enumerate(tiles):
            pass
        for half in range(2):
            for piece in range(4):
                wb = next_wst()
                C.dma("pool", wb[:, :, :], w_ff2[l, piece * 1024:(piece + 1) * 1024, half * 512:(half + 1) * 512]
                      .rearrange("(k p) n -> p k n", p=128))
                for j, t in enumerate(tiles):
                    pzb = next_pz()
                    for kc in range(8):
                        MM(pzb[:, :], a2[:, piece * 8 + kc, j * 128:(j + 1) * 128], wb[:, kc, :], start=(kc == 0), stop=(kc == 7))
                    dst = YAC[:, j, half * 512:(half + 1) * 512]
                    if piece == 0:
                        CPY(dst, pzb[:, :])
                    else:
                        CPY(sq[:, 0:512], pzb[:, :])
                        TT(dst, dst, sq[:, 0:512], ALU.add)
        for j, t in enumerate(tiles):
            CPY(obuf[:, :], YAC[:, j, :], eng="dve")
            residual_update(l, t, 1)


    kstop = int(os.environ.get("KSTOP", "99"))
    for l in range(int(os.environ.get('KLAYERS', DEPTH))):
        compute_mod(l)
        load_mod(l, 0)
        load_mod(l, 1)
        C.dma("sp", bfox_t[:, :], b_fox[l:l + 1, :].to_broadcast([128, 6]))
        MEMSET(kcmT2[:, :, :], 0.0)
        MEMSET(V_fox[:, :, :, 64:65], 1.0)
        C.dma("pool", wp_sb[:, :, :], w_pool[l].rearrange("g c d -> c g d"))
        C.dma("sp", psc[:, :], pool_scale[l:l + 1, :].to_broadcast([128, 256]))
        blocks = [[0, 1, 2, 3], [4, 5, 6, 7], [8, 9, 10, 11], [12, 13, 14, 15], [16]]
        for Q, tiles in enumerate(blocks):
            if Q == 4:
                C.barrier()
            for j, t in enumerate(tiles):
                src = x_all[t * 128:(t + 1) * 128, :] if l == 0 else y_all.p(t)[t * 128:(t + 1) * 128, :]
                norm_mod_T(src, t, j, 0)
            for g in range(7):
                n = G_SZ[g]
                c0 = G_OFF[g]
                wb = next_wst()
                C.dma("pool", wb[:, :, 0:n], w_in[l, :, c0:c0 + n].rearrange("(k p) n -> p k n", p=128))
                for j, t in enumerate(tiles):
                    pzb = next_pz()
                    for kc in range(8):
                        MM(pzb[:, 0:n], hT4[:, kc, j * 128:(j + 1) * 128], wb[:, kc, 0:n], start=(kc == 0), stop=(kc == 7))
                    post(l, g, t, j, pzb)
            if Q < 4:
                attention_block(l, Q)
                if DBG and l == 0:
                    for j, t in enumerate(tiles):
                        C.dma("sp", dbg_ocat[t * 128:(t + 1) * 128, :], OC[:, j, :])
                if kstop <= 4:
                    continue
                outproj_block(l, tiles, None, None, 0)
            else:
                decode_block(l)
                if DBG and l == 0:
                    C.dma("sp", dbg_ocat[NT * 128:(NT + 1) * 128, :], OC[:, 0, :])
                outproj_block(l, tiles, None, None, 0)
        C.dma("sp", o_logf[l, :, :, :], LOGF[:, :, :])
        if kstop <= 5:
            break
        C.barrier()
        for Q, tiles in enumerate(blocks):
            mlp_block(l, tiles, A2)
        C.barrier()
    C.finish()
    es.close()
    print("instructions:", C.n_ins)
    return nc


def _pack(parts):
    offs = {}
    cols = []
    o = 0
    for k, v in parts.items():
        offs[k] = o
        a = np.zeros((128, v.shape[1]), np.float32)
        a[:v.shape[0]] = v
        cols.append(a)
        o += v.shape[1]
    return offs, np.ascontiguousarray(np.concatenate(cols, axis=1))


def _make_consts():
    f = np.float32
    p = np.arange(128)
    parts = {}
    inv = 500000.0 ** (-np.arange(0, 16, 2, dtype=f) / 16.0)
    pos = np.zeros((128, NTT), f)
    for t in range(NT):
        pos[:, t] = t * 128 + p
    pos[:, NT] = 8192.0
    ang = pos[:, :, None] * inv[None, None, :]
    parts["cos"] = np.repeat(np.cos(ang).astype(f)[:, :, None, :], 6, axis=2).reshape(128, -1)
    parts["sin"] = np.repeat(np.sin(ang).astype(f)[:, :, None, :], 6, axis=2).reshape(128, -1)
    selA = np.zeros((128, 16, 32), f)
    selB = np.zeros((128, 16, 32), f)
    jj = np.arange(32)
    for t in range(16):
        cur = (t * 128 + p) // 64
        valid = (jj[None, :] <= cur[:, None])
        forced = (jj[None, :] == 0) | (jj[None, :] == cur[:, None]) | (jj[None, :] == cur[:, None] - 1)
        selA[:, t] = (valid & ~forced).astype(f)
        selB[:, t] = 1e4 * forced.astype(f) - (~valid).astype(f)
    parts["selA"] = selA.reshape(128, -1)
    parts["selB"] = selB.reshape(128, -1)
    parts["identf"] = np.eye(128, dtype=f)
    parts["pidx"] = p.astype(f).reshape(128, 1)
    negrow = np.full((128, 4), NEGB, f)
    for b_ in range(4):
        negrow[b_, b_] = 0.0
    parts["negrow"] = negrow
    neg0 = np.zeros((128, 1), f)
    neg0[0, 0] = NEGB
    parts["neg0"] = neg0
    co, cst = _pack(parts)
    pb = {}
    pb["ident"] = np.eye(128, dtype=f)
    pb["tri_le"] = (p[:, None] <= p[None, :]).astype(f)
    pb["tri_gt"] = (p[:, None] > p[None, :]).astype(f)
    ex = np.zeros((32, 16, 128), f)
    avg = np.zeros((128, 16, 32), f)
    for t in range(16):
        blk = (t * 128 + p) // 64
        ex[blk, t, p] = 1.0
        avg[p, t, blk] = 1.0 / 64
    pb["expand"] = ex.reshape(32, -1)
    pb["avg"] = avg.reshape(128, -1)
    q = np.arange(T)
    pb["cmpm"] = ((64 * (jj[:, None] + 1) - 1) <= q[None, :]).astype(f)
    for wi, w in enumerate((2, 4, 8, 16)):
        d = p[None, :] - p[:, None]
        cur = ((d >= 0) & (d < w)).astype(f) / w - np.eye(128, dtype=f)
        cnt = np.minimum(w, p + 1).astype(f)
        cur0 = ((d >= 0) & (d < w)).astype(f) / cnt[None, :] - np.eye(128, dtype=f)
        prev = ((p[:, None] - 128) > (p[None, :] - w)).astype(f) / w
        pb["bcur_%d" % wi] = cur
        pb["bcur0_%d" % wi] = cur0
        pb["bprev_%d" % wi] = prev
    aw = np.zeros((128, 254), f)
    aw[:64, 126] = 1.0 / 64
    aw[64:, 127] = 1.0 / 64
    pb["avgwin"] = aw
    oneh = np.zeros((2, 256), f)
    oneh[0, 0:128] = 1.0
    oneh[1, 128:256] = 1.0
    pb["oneh"] = oneh
    cob, cstb = _pack(pb)
    return co, cst, cob, cstb


CO, CST_NP, COB, CSTB_NP = _make_consts()
CST_N = CST_NP.shape[1]
CSTB_N = CSTB_NP.shape[1]

_NC_CACHE = {}


def kernel(x_prompt, x_sample, cache_fox_kv, cache_fox_logf, cache_nsa_kv, cache_nsa_win, state_pool,
           page_table, c_prompt, c_sample, w_ada, b_ada, norm_g, w_in, b_fox_f, w_out, w_pool,
           pool_scale, w_ff1, w_ff2, _stage=99, _cores=8):
    f = np.float32
    npool = int(np.asarray(cache_fox_kv).shape[1])
    ncores = int(_cores)
    key = (_stage, npool)
    if key not in _NC_CACHE:
        _NC_CACHE[key] = build(_stage, npool)
    nc = _NC_CACHE[key]
    cfox = np.ascontiguousarray(np.asarray(cache_fox_kv, f)).reshape(DEPTH, npool * 128, 768)
    clogf = np.ascontiguousarray(np.asarray(cache_fox_logf, f)).reshape(DEPTH, npool * 128, 6)
    cnsa = np.ascontiguousarray(np.asarray(cache_nsa_kv, f)).reshape(DEPTH, npool * 128, 512)
    ptab = np.asarray(page_table, np.int32)
    in_maps = []
    for i in range(ncores):
        xa = np.zeros((NTT * 128, D), f)
        xa[:T] = x_prompt[i]
        xa[T:T + NS] = x_sample[4 * i:4 * i + 4, 0]
        c5 = np.concatenate([c_prompt[i:i + 1], c_sample[4 * i:4 * i + 4]], axis=0)
        c5T = np.ascontiguousarray(c5.T.reshape(8, 128, 5).transpose(1, 0, 2))
        in_maps.append({
            "x_all": xa, "c5T": c5T, "w_ada": np.asarray(w_ada, f), "b_ada": np.asarray(b_ada, f),
            "norm_g": np.asarray(norm_g, f), "w_in": np.asarray(w_in, f), "b_fox": np.asarray(b_fox_f, f),
            "w_out": np.asarray(w_out, f), "w_pool": np.asarray(w_pool, f), "pool_scale": np.asarray(pool_scale, f),
            "w_ff1": np.asarray(w_ff1, f), "w_ff2": np.asarray(w_ff2, f), "cst": CST_NP, "cstb": CSTB_NP,
            "win_in": np.ascontiguousarray(np.asarray(cache_nsa_win, f)[:, 4 * i:4 * i + 4].reshape(DEPTH, NS, 512, 256)),
            "pool_in": np.ascontiguousarray(np.asarray(state_pool, f)[:, 4 * i:4 * i + 4]),
            "c_fox0": cfox[0], "c_fox1": cfox[1], "c_logf0": clogf[0], "c_logf1": clogf[1],
            "c_nsa0": cnsa[0], "c_nsa1": cnsa[1],
            "ptab": np.ascontiguousarray(ptab[4 * i:4 * i + 4].reshape(1, NS * NPAGE)),
        })
    res = run_bass_kernel_spmd(nc, in_maps, core_ids=list(range(ncores)))
    if _stage != 99:
        return res.results
    R = res.results
    B, BS = 8, 32
    yp = np.zeros((B, T, D), f)
    ys = np.zeros((BS, 1, D), f)
    fox_kv_p = np.zeros((DEPTH, B, T, 2, 6, 64), f)
    fox_logf_p = np.zeros((DEPTH, B, T, 6), f)
    nsa_kv_p = np.zeros((DEPTH, B, T, 4, 2, 64), f)
    nsa_win_p = np.zeros((DEPTH, B, 512, 2, 2, 64), f)
    pool_p = np.zeros((DEPTH, B, 15, 256), f)
    fox_kv_s = np.zeros((DEPTH, BS, 1, 2, 6, 64), f)
    fox_logf_s = np.zeros((DEPTH, BS, 1, 6), f)
    nsa_kv_s = np.zeros((DEPTH, BS, 1, 4, 2, 64), f)
    nsa_win_s = np.zeros((DEPTH, BS, 512, 2, 2, 64), f)
    pool_s = np.zeros((DEPTH, BS, 15, 256), f)
    for i in range(ncores):
        r = R[i]
        sl = slice(4 * i, 4 * i + 4)
        yp[i] = r["y_all"][:T]
        ys[sl, 0] = r["y_all"][T:T + NS]
        fk = r["o_foxkv"]
        fox_kv_p[:, i] = fk[:, :T].reshape(DEPTH, T, 2, 6, 64)
        fox_kv_s[:, sl, 0] = fk[:, T:T + NS].reshape(DEPTH, NS, 2, 6, 64)
        lf = r["o_logf"]
        fox_logf_p[:, i] = lf[:, :, :NT].transpose(0, 2, 1, 3).reshape(DEPTH, T, 6)
        fox_logf_s[:, sl, 0] = lf[:, 0:NS, NT]
        nk = r["o_nsakv"]
        nsa_kv_p[:, i] = nk[:, :T].reshape(DEPTH, T, 4, 2, 64)
        nsa_kv_s[:, sl, 0] = nk[:, T:T + NS].reshape(DEPTH, NS, 4, 2, 64)
        nsa_win_p[:, i] = r["o_win"][:, :512].reshape(DEPTH, 512, 2, 2, 64)
        pool_p[:, i] = r["o_u"][:, 113:128]
        nsa_win_s[:, sl] = r["o_win_s"].reshape(DEPTH, NS, 512, 2, 2, 64)
        pool_s[:, sl] = r["o_pool_s"]
    return (yp, ys, fox_kv_p, fox_logf_p, nsa_kv_p, nsa_win_p, pool_p,
            fox_kv_s, fox_logf_s, nsa_kv_s, nsa_win_s, pool_s)
```
